# Optimizing a Trainium2 kernel written in Bass

```python
import math
import jax, jax.numpy as jnp
from jax import lax
import numpy as np

D_MODEL = 2048
BATCH = 1
SEQ = 8192
DEPTH = 1

GRID_W = 64
CTX_LEN = 256
N_DIFF_HEADS = D_MODEL // 256
HEAD_DIM = 64
V_HEAD_DIM = 2 * HEAD_DIM
ATTN_WIDTH = N_DIFF_HEADS * V_HEAD_DIM
SGU_WIDTH = D_MODEL // 2
N_SGU_GROUPS = 8
SGU_GROUP_DIM = SGU_WIDTH // N_SGU_GROUPS
CHUNK = 128
Q_BLOCK = 128
FF_DIM = 4 * D_MODEL
IN_WIDTH = 3 * ATTN_WIDTH + 2 * SGU_WIDTH
ROPE_BASE = 10000.0
EPS = 1e-6

kernel_name = "hybrid_diffattn_sgu_dit_block"


def rms_norm(x, g):
    xf = x.astype(jnp.float32)
    y = xf * lax.rsqrt(jnp.mean(xf * xf, axis=-1, keepdims=True) + EPS)
    return (y * g.astype(jnp.float32)).astype(x.dtype)


def layer_norm(x, g, b):
    xf = x.astype(jnp.float32)
    mu = jnp.mean(xf, axis=-1, keepdims=True)
    var = jnp.mean(jnp.square(xf - mu), axis=-1, keepdims=True)
    y = (xf - mu) * lax.rsqrt(var + EPS)
    return (y * g.astype(jnp.float32) + b.astype(jnp.float32)).astype(x.dtype)


def modulated_norm(x, g, shift, scale):
    return rms_norm(x, g) * (1 + scale) + shift


def axial_rope_tables(n_tokens):
    rows = n_tokens // GRID_W
    r, col = jnp.meshgrid(jnp.arange(rows, dtype=jnp.float32),
                          jnp.arange(GRID_W, dtype=jnp.float32), indexing="ij")
    n_freq = HEAD_DIM // 4
    inv = ROPE_BASE ** (-jnp.arange(n_freq, dtype=jnp.float32) / n_freq)
    ang = jnp.concatenate([r.reshape(-1, 1) * inv, col.reshape(-1, 1) * inv], axis=-1)
    return jnp.cos(ang), jnp.sin(ang)


def apply_rope(t, cos, sin):
    tf = t.astype(jnp.float32).reshape(t.shape[:-1] + (HEAD_DIM // 2, 2))
    t1, t2 = tf[..., 0], tf[..., 1]
    cs, sn = cos[None, :, None, :], sin[None, :, None, :]
    out = jnp.stack([t1 * cs - t2 * sn, t1 * sn + t2 * cs], axis=-1)
    return out.reshape(t.shape).astype(t.dtype)


def diff_attention(q, k, v, lam, lam_init, subln_g):
    B, Nq = q.shape[0], q.shape[1]
    Nk = k.shape[1]
    nb = Nq // Q_BLOCK
    scale = HEAD_DIM ** -0.5
    qb = q.reshape(B, nb, Q_BLOCK, 2 * N_DIFF_HEADS, HEAD_DIM).transpose(1, 0, 2, 3, 4)

    def one_block(qblk):
        s = jnp.einsum("bqhd,bkhd->bhqk", qblk, k).astype(jnp.float32) * scale
        p = jax.nn.softmax(s, axis=-1).reshape(B, N_DIFF_HEADS, 2, Q_BLOCK, Nk)
        a = p[:, :, 0] - lam * p[:, :, 1]
        return jnp.einsum("bhqk,bkhe->bqhe", a.astype(v.dtype), v)

    o = lax.map(one_block, qb)
    o = o.transpose(1, 0, 2, 3, 4).reshape(B, Nq, N_DIFF_HEADS, V_HEAD_DIM)
    o = rms_norm(o, subln_g) * (1 - lam_init)
    return o.reshape(B, Nq, ATTN_WIDTH)


def chunk_sgu(u, v, ln_g, ln_b, w_s, b_s):
    B, N, _ = v.shape
    vn = layer_norm(v, ln_g, ln_b).reshape(B, N // CHUNK, CHUNK, N_SGU_GROUPS, SGU_GROUP_DIM)
    mixed = jnp.einsum("gij,bnjgc->bnigc", w_s, vn) + b_s.T[None, None, :, :, None]
    return u * mixed.reshape(B, N, SGU_WIDTH)


def gated_merge(h, attn_o, sgu_o, w_gate, b_gate, w_br_attn, w_br_sgu, w_out):
    gates = jax.nn.sigmoid((h @ w_gate + b_gate).astype(jnp.float32)).astype(h.dtype)
    g_attn, g_sgu = jnp.split(gates, 2, axis=-1)
    return (g_attn * (attn_o @ w_br_attn) + g_sgu * (sgu_o @ w_br_sgu)) @ w_out


def sq_relu_ffn(h, w1, w2):
    return jnp.square(jax.nn.relu(h @ w1)) @ w2


def setup_inputs(seed: int = 0) -> dict:
    key = jax.random.key(seed)
    ks = jax.random.split(key, 26)
    L = DEPTH

    def nrm(k, shape, s):
        return jax.random.normal(k, shape, jnp.float32) * s

    return {
        "x": nrm(ks[0], (BATCH, SEQ, D_MODEL), 1.0),
        "c": nrm(ks[1], (BATCH, D_MODEL), 1.0),
        "ctx": nrm(ks[2], (BATCH, CTX_LEN, D_MODEL), 1.0),
        "c_ctx": nrm(ks[3], (D_MODEL,), 1.0),
        "w_ada": nrm(ks[4], (L, D_MODEL, 6 * D_MODEL), 0.5 * D_MODEL ** -0.5),
        "b_ada": nrm(ks[5], (L, 6 * D_MODEL), 0.01),
        "norm1_g": 1.0 + nrm(ks[6], (L, D_MODEL), 0.02),
        "norm2_g": 1.0 + nrm(ks[7], (L, D_MODEL), 0.02),
        "w_in": nrm(ks[8], (L, D_MODEL, IN_WIDTH), D_MODEL ** -0.5),
        "lam_q1": nrm(ks[9], (L, HEAD_DIM), 0.1),
        "lam_k1": nrm(ks[10], (L, HEAD_DIM), 0.1),
        "lam_q2": nrm(ks[11], (L, HEAD_DIM), 0.1),
        "lam_k2": nrm(ks[12], (L, HEAD_DIM), 0.1),
        "subln_g": 1.0 + nrm(ks[13], (L, V_HEAD_DIM), 0.02),
        "sgu_ln_g": 1.0 + nrm(ks[14], (L, SGU_WIDTH), 0.02),
        "sgu_ln_b": nrm(ks[15], (L, SGU_WIDTH), 0.02),
        "w_spatial": nrm(ks[16], (L, N_SGU_GROUPS, CHUNK, CHUNK), 0.5 * CHUNK ** -0.5),
        "b_spatial": 1.0 + nrm(ks[17], (L, N_SGU_GROUPS, CHUNK), 0.02),
        "w_gate": nrm(ks[18], (L, D_MODEL, 2 * D_MODEL), D_MODEL ** -0.5),
        "b_gate": nrm(ks[19], (L, 2 * D_MODEL), 0.01),
        "w_br_attn": nrm(ks[20], (L, ATTN_WIDTH, D_MODEL), ATTN_WIDTH ** -0.5),
        "w_br_sgu": nrm(ks[21], (L, SGU_WIDTH, D_MODEL), SGU_WIDTH ** -0.5),
        "w_out": nrm(ks[22], (L, D_MODEL, D_MODEL), D_MODEL ** -0.5),
        "w_ff1": nrm(ks[23], (L, D_MODEL, FF_DIM), D_MODEL ** -0.5),
        "w_ff2": nrm(ks[24], (L, FF_DIM, D_MODEL), FF_DIM ** -0.5),
        "final_g": 1.0 + nrm(ks[25], (D_MODEL,), 0.02),
    }


def reference(x, c, ctx, c_ctx, w_ada, b_ada, norm1_g, norm2_g, w_in,
              lam_q1, lam_k1, lam_q2, lam_k2, subln_g, sgu_ln_g, sgu_ln_b,
              w_spatial, b_spatial, w_gate, b_gate, w_br_attn, w_br_sgu, w_out,
              w_ff1, w_ff2, final_g):
    B, N, _ = x.shape
    Lc = ctx.shape[1]
    A = ATTN_WIDTH
    H2 = 2 * N_DIFF_HEADS
    cos, sin = axial_rope_tables(N)
    cx = ctx
    for l in range(DEPTH):
        lam_init = 0.8 - 0.6 * math.exp(-0.3 * l)
        lam = (jnp.exp(jnp.sum((lam_q1[l] * lam_k1[l]).astype(jnp.float32)))
               - jnp.exp(jnp.sum((lam_q2[l] * lam_k2[l]).astype(jnp.float32))) + lam_init)
        mod_x = jax.nn.silu(c) @ w_ada[l] + b_ada[l]
        mod_c = jax.nn.silu(c_ctx) @ w_ada[l] + b_ada[l]
        sh1, sc1, g1, sh2, sc2, g2 = [m[:, None, :] for m in jnp.split(mod_x, 6, axis=-1)]
        csh1, csc1, cg1, csh2, csc2, cg2 = jnp.split(mod_c, 6, axis=-1)

        hx = modulated_norm(x, norm1_g[l], sh1, sc1)
        hc = modulated_norm(cx, norm1_g[l], csh1, csc1)
        w_l = w_in[l]
        proj_x = hx @ w_l
        qx = apply_rope(proj_x[..., :A].reshape(B, N, H2, HEAD_DIM), cos, sin)
        kx = apply_rope(proj_x[..., A:2 * A].reshape(B, N, H2, HEAD_DIM), cos, sin)
        vx = proj_x[..., 2 * A:3 * A].reshape(B, N, N_DIFF_HEADS, V_HEAD_DIM)
        ux, vsx = jnp.split(jax.nn.gelu(proj_x[..., 3 * A:]), 2, axis=-1)
        kc = (hc @ w_l[:, A:2 * A]).reshape(B, Lc, H2, HEAD_DIM)
        vc = (hc @ w_l[:, 2 * A:3 * A]).reshape(B, Lc, N_DIFF_HEADS, V_HEAD_DIM)
        k_all = jnp.concatenate([kc, kx], axis=1)
        v_all = jnp.concatenate([vc, vx], axis=1)
        ax = diff_attention(qx, k_all, v_all, lam, lam_init, subln_g[l])
        sx = chunk_sgu(ux, vsx, sgu_ln_g[l], sgu_ln_b[l], w_spatial[l], b_spatial[l])
        x = x + g1 * gated_merge(hx, ax, sx, w_gate[l], b_gate[l], w_br_attn[l], w_br_sgu[l], w_out[l])

        x = x + g2 * sq_relu_ffn(modulated_norm(x, norm2_g[l], sh2, sc2), w_ff1[l], w_ff2[l])

        if l < DEPTH - 1:
            qc = (hc @ w_l[:, :A]).reshape(B, Lc, H2, HEAD_DIM)
            uc, vsc = jnp.split(jax.nn.gelu(hc @ w_l[:, 3 * A:]), 2, axis=-1)
            ac = diff_attention(qc, kc, vc, lam, lam_init, subln_g[l])
            scx = chunk_sgu(uc, vsc, sgu_ln_g[l], sgu_ln_b[l], w_spatial[l], b_spatial[l])
            cx = cx + cg1 * gated_merge(hc, ac, scx, w_gate[l], b_gate[l], w_br_attn[l], w_br_sgu[l], w_out[l])
            cx = cx + cg2 * sq_relu_ffn(modulated_norm(cx, norm2_g[l], csh2, csc2), w_ff1[l], w_ff2[l])
    return rms_norm(x, final_g)
```

```python
import bisect
import os
import contextlib
import itertools
import numpy as np
import concourse.bass as bass
import concourse.mybir as mybir
from concourse.bass_utils import run_bass_kernel_spmd

F32 = mybir.dt.float32
BF16 = mybir.dt.bfloat16
AF = mybir.ActivationFunctionType
ALU = mybir.AluOpType
AX = mybir.AxisListType

NCORES = 8
D = 2048
T = 1024
NT = 8
CT = 256
A = 1024
EPS = 1e-6
NSEM_DMA = 10
UNIT = 4096
NUNIT = 6
STRICT = bool(int(os.environ.get('KSTRICT', '0')))


class Arena:
    def __init__(self):
        self.b = [0, 1 << 40]
        self.st = [[None, {}, []]]

    def _cut(self, pos):
        i = bisect.bisect_right(self.b, pos) - 1
        if self.b[i] == pos:
            return
        w, rc, rd = self.st[i]
        self.b.insert(i + 1, pos)
        self.st.insert(i + 1, [w, dict(rc), list(rd)])

    def cover(self, lo, hi):
        self._cut(lo)
        self._cut(hi)
        i = bisect.bisect_left(self.b, lo)
        j = bisect.bisect_left(self.b, hi)
        return self.st[i:j]


class Op:
    __slots__ = ("eng", "kind", "fn", "deps", "waits", "signal", "ev")


class Prog:
    ENG = ["pe", "act", "dve", "pool", "sp"]

    def __init__(self):
        self.ops = []
        self.arenas = {}
        self.whole = set()

    def track(self, name):
        self.arenas[name] = Arena()

    def track_whole(self, name):
        self.arenas[name] = Arena()
        self.whole.add(name)

    def _runs(self, ap):
        name = ap.tensor.name
        if name not in self.arenas:
            return None, ()
        if name in self.whole:
            return name, [(0, 1)]
        esz = 4 if ap.dtype == F32 else 2
        dims = list(ap.ap)
        pstep = dims[0][0]
        off = ap.offset
        f0 = off % pstep if pstep > 0 else off
        free = sorted((s, c) for (s, c) in dims[1:] if c > 1 and s != 0)
        if not free:
            return name, [(f0 * esz, (f0 + 1) * esz)]
        s0, c0 = free[0]
        span = (c0 - 1) * s0 + 1
        outer = free[1:]
        while outer and outer[0][0] == span:
            span *= outer[0][1]
            outer = outer[1:]
        n = 1
        for s, c in outer:
            n *= c
        if n > 64:
            hi = f0 + span + sum((c - 1) * s for s, c in outer)
            return name, [(f0 * esz, hi * esz)]
        runs = []
        for idx in itertools.product(*[range(c) for s, c in outer]):
            st = f0 + sum(i * s for i, (s, c) in zip(idx, outer))
            runs.append((st * esz, (st + span) * esz))
        return name, runs

    def add(self, eng, kind, fn, reads, writes):
        oid = len(self.ops)
        deps = {}
        rcov, wcov = [], []
        for ap in reads:
            name, runs = self._runs(ap)
            for lo, hi in runs:
                rcov.extend(self.arenas[name].cover(lo, hi))
        for ap in writes:
            name, runs = self._runs(ap)
            for lo, hi in runs:
                wcov.extend(self.arenas[name].cover(lo, hi))
        for st in rcov:
            if st[0] is not None:
                deps[st[0]] = 2
        for st in wcov:
            if st[0] is not None:
                deps.setdefault(st[0], 1)
            for o in st[1].values():
                deps.setdefault(o, 1)
            for o in st[2]:
                deps.setdefault(o, 1)
        for st in rcov:
            if kind == "c":
                st[1][eng] = oid
            else:
                st[2].append(oid)
        for st in wcov:
            st[0] = oid
            st[1] = {}
            st[2] = []
        deps.pop(oid, None)
        op = Op()
        op.eng, op.kind, op.fn, op.deps = eng, kind, fn, deps
        op.waits, op.signal, op.ev = [], False, None
        self.ops.append(op)
        return oid

    def op(self, eng, method, reads, writes, *args, **kw):
        def fn(e, method=method, args=args, kw=kw):
            return getattr(e, method)(*args, **kw)
        return self.add(eng, "c", fn, reads, writes)

    def mm(self, items):
        def fn(e, items=items):
            ins = None
            for (o, l, r, st, sp) in items:
                ins = e.matmul(o, lhsT=l, rhs=r, start=st, stop=sp)
            return ins
        reads = []
        for it in items:
            reads.append(it[1])
            reads.append(it[2])
        writes = [it[0] for it in items]
        return self.add("pe", "c", fn, reads, writes)

    def tr(self, items):
        def fn(e, items=items):
            ins = None
            for (o, i, idn) in items:
                ins = e.transpose(out=o, in_=i, identity=idn)
            return ins
        reads = []
        for it in items:
            reads.append(it[1])
            reads.append(it[2])
        return self.add("pe", "c", fn, reads, [it[0] for it in items])

    def act(self, out, in_, func, bias=None, scale=None, accum=None):
        kw = {}
        reads = [in_]
        writes = [out]
        if bias is not None:
            kw["bias"] = bias
            if not isinstance(bias, float):
                reads.append(bias)
        if scale is not None:
            kw["scale"] = scale
            if not isinstance(scale, float):
                reads.append(scale)
        if accum is not None:
            kw["accum_out"] = accum
            writes.append(accum)
        return self.op("act", "activation", reads, writes, out=out, in_=in_, func=func, **kw)

    def tt(self, eng, out, in0, in1, op):
        return self.op(eng, "tensor_tensor", [in0, in1], [out], out=out, in0=in0, in1=in1, op=op)

    def ts(self, eng, out, in0, s1, s2, op0, op1=None):
        reads = [in0] + [s for s in (s1, s2) if s is not None and not isinstance(s, float)]
        kw = dict(out=out, in0=in0, scalar1=s1, scalar2=s2, op0=op0)
        if op1 is not None:
            kw["op1"] = op1
        return self.op(eng, "tensor_scalar", reads, [out], **kw)

    def stt(self, eng, out, in0, scalar, in1, op0, op1):
        reads = [in0, in1] + ([] if isinstance(scalar, float) else [scalar])
        return self.op(eng, "scalar_tensor_tensor", reads, [out], out=out, in0=in0, scalar=scalar,
                       in1=in1, op0=op0, op1=op1)

    def copy(self, eng, out, in_):
        if eng == "act":
            return self.op("act", "copy", [in_], [out], out=out, in_=in_)
        return self.op(eng, "tensor_copy", [in_], [out], out=out, in_=in_)

    def dma(self, q, out, in_, extra_reads=()):
        def fn(e, out=out, in_=in_):
            return e.dma_start(out=out, in_=in_)
        return self.add(q, "d", fn, [in_] + list(extra_reads), [out])

    def finalize(self):
        ops = self.ops
        waitedc = {e: {} for e in self.ENG}
        waitedd = {e: set() for e in self.ENG}
        ndma = {e: 0 for e in self.ENG}
        prev = {e: [None] * NSEM_DMA for e in self.ENG}
        self.dma_final = {}
        for oid, op in enumerate(ops):
            W = []
            best = {}
            for p, typ in op.deps.items():
                P = ops[p]
                if P.kind == "c":
                    if op.kind == "c" and op.eng == P.eng:
                        if op.eng == "pe" or not (typ == 2 or STRICT):
                            continue
                    if waitedc[op.eng].get(P.eng, -1) >= p:
                        continue
                    best[P.eng] = max(best.get(P.eng, -1), p)
                else:
                    if p in waitedd[op.eng]:
                        continue
                    W.append(p)
            if op.kind == "d":
                q = op.eng
                k = ndma[q] % NSEM_DMA
                pv = prev[q][k]
                if pv is not None and pv not in waitedd[q] and pv not in W:
                    W.append(pv)
                op.ev = (q, k, 16 * (ndma[q] // NSEM_DMA + 1))
                prev[q][k] = oid
                ndma[q] += 1
                self.dma_final[(q, k)] = op.ev[2]
            for e, p in best.items():
                W.append(p)
                waitedc[op.eng][e] = p
            for p in W:
                if ops[p].kind != "c":
                    waitedd[op.eng].add(p)
                ops[p].signal = True
            op.waits = W
        cnt = {e: 0 for e in self.ENG}
        ncc = 0
        for op in ops:
            if op.kind == "c" and op.signal:
                cnt[op.eng] += 1
                op.ev = ("E", op.eng, cnt[op.eng])
            elif op.kind == "cc":
                ncc += 1
                op.ev = ("CC", None, ncc)
        self.ncc = ncc

    def emit(self, name, e, esem, dsem, ccsem):
        def sem_of(ev):
            if ev[0] == "E":
                return esem[ev[1]], ev[2]
            if ev[0] == "CC":
                return ccsem, ev[2]
            return dsem[(ev[0], ev[1])], ev[2]
        for op in self.ops:
            if op.eng != name:
                continue
            for p in op.waits:
                s, v = sem_of(self.ops[p].ev)
                e.wait_ge(s, v)
            ins = op.fn(e)
            if op.kind == "d":
                s, v = sem_of(op.ev)
                ins.then_inc(s, 16)
            elif op.kind == "cc":
                ins.then_inc(ccsem)
            elif op.signal:
                ins.then_inc(esem[name], 1)
        if name == "sp":
            for (q, k), v in self.dma_final.items():
                e.wait_ge(dsem[(q, k)], v)
            if self.ncc:
                e.wait_ge(ccsem, self.ncc)


class Ring:
    def __init__(self, P, Wt, Xext=None):
        self.P, self.Wt, self.Xext = P, Wt, Xext
        self.blocks = []
        self.owner = [None] * 10
        self.ptr = 0
        self.next_load = 0
        self.nunits = NUNIT
        self.ext_ok = False
        self.grow_at = None
        self.shrink_at = None

    def declare(self, n, loads, post=None, unit=None):
        self.blocks.append(dict(n=n, loads=loads, post=post, unit=None, fixed=unit))
        return len(self.blocks) - 1

    def base(self, bid):
        b = self.blocks[bid]
        u = b["unit"]
        if u < NUNIT:
            return self.Wt[:, u * UNIT:(u + b["n"]) * UNIT]
        return self.Xext[:, (u - NUNIT) * UNIT:(u - NUNIT + b["n"]) * UNIT]

    def pump(self):
        while self.next_load < len(self.blocks):
            b = self.blocks[self.next_load]
            fixed = b["fixed"]
            u = self.ptr if fixed is None else fixed
            if fixed is None and u + b["n"] > self.nunits:
                assert u == self.nunits
                u = 0
            if any(self.owner[u + i] is not None for i in range(b["n"])):
                break
            if u >= NUNIT and not self.ext_ok:
                break
            b["unit"] = u
            for i in range(b["n"]):
                self.owner[u + i] = self.next_load
            if fixed is None:
                self.ptr = u + b["n"]
                if self.ptr >= self.nunits:
                    self.ptr = 0
            base = self.base(self.next_load)
            for vf, src in b["loads"]:
                self.P.dma("pool", vf(base), src)
            if b["post"] is not None:
                b["post"](base)
            self.next_load += 1

    def get(self, bid):
        self.pump()
        assert self.blocks[bid]["unit"] is not None, bid
        return self.base(bid)

    def release(self, bid):
        b = self.blocks[bid]
        for i in range(b["n"]):
            self.owner[b["unit"] + i] = None
        self.pump()


def kview(base, kc, cols):
    return base.rearrange("p (k c) -> p k c", k=kc)[:, :, :] if cols is None else base[:, 0:kc * cols].rearrange("p (k c) -> p k c", k=kc)


def wsrc(w_ap, r0, nrows, c0, ncols):
    return w_ap[r0:r0 + nrows, c0:c0 + ncols].rearrange("(k p) c -> p k c", p=128)


def build_nc():
    nc = bass.Bass("TRN2", target_bir_lowering=False)
    KFAST = bool(int(os.environ.get('KFAST', '0')))
    KSUB = int(os.environ.get('KSUB', '99'))
    def dt(n, s, d=F32):
        if KFAST and n.startswith('w_'):
            s = [128, 128]
        return nc.dram_tensor(n, s, d, kind="ExternalInput").ap()
    x_d = dt("x", [T, D])
    ctx_d = dt("ctx", [CT, D])
    vecs_d = dt("vecs", [128, 208])
    lam_d = dt("lamv", [128, 256])
    subg_d = dt("subg", [128, 128])
    fing_d = dt("fing", [128, D])
    wsT_d = dt("wsT", [128, 8 * 128])
    bsb_d = dt("bsb", [128, 8 * 128])
    cos_d = dt("ropec", [128, 8 * 32])
    sin_d = dt("ropes", [128, 8 * 32])
    idn_d = dt("idn", [128, 128])
    w_ada = dt("w_ada", [D, 1536])
    w_in = dt("w_in", [D, 5 * A])
    w_gate = dt("w_gate", [D, 2 * D])
    w_bra = dt("w_br_attn", [A, D])
    w_brs = dt("w_br_sgu", [A, D])
    w_out = dt("w_out", [D, D])
    w_ff1 = dt("w_ff1", [D, 4 * D])
    w_ff2 = dt("w_ff2", [4 * D, D])
    out_d = nc.dram_tensor("out", [T, D], F32, kind="ExternalOutput").ap()
    kv_loc = nc.dram_tensor("kv_loc", [2048, 1024], BF16)
    mod_loc = nc.dram_tensor("mod_loc", [2, 1536], F32)
    mod_all = nc.dram_tensor("mod_all", [2 * NCORES, 1536], F32)
    kv_all = nc.dram_tensor("kv_all", [NCORES * 2048, 1024], BF16)

    P = Prog()
    KSTOP = int(os.environ.get('KSTOP', '99'))
    KDUMP = os.environ.get('KDUMP', 'X')
    es = contextlib.ExitStack()
    with es:
        def sb(name, shape, dtype):
            t = es.enter_context(nc.sbuf_tensor("sb_" + name, shape, dtype))
            P.track("sb_" + name)
            return t
        X = sb("X", [128, 32768], BF16)
        H = sb("H", [128, 16384], BF16)
        M = sb("M", [128, 16384], BF16)
        Wt = sb("Wt", [128, NUNIT * UNIT], BF16)
        S = sb("S", [128, 4096], F32)
        idf = sb("idf", [128, 128], F32)
        idb = sb("idb", [128, 128], BF16)
        onesf = sb("onesf", [128, 128], F32)
        onesb = sb("onesb", [128, 128], BF16)
        vecs = sb("vecs", [128, 208], F32)
        gsub = sb("gsub", [128, 128], F32)
        cos_sb = sb("cos_sb", [128, 8, 32], F32)
        sin_sb = sb("sin_sb", [128, 8, 32], F32)
        modT = sb("modT", [128, 96, 2], F32)
        sT = sb("sT", [128, 16, 2], BF16)
        amod = sb("amod", [128, 3, 16], F32)
        stat = sb("stat", [128, 64], F32)
        Vc = sb("Vc", [128, 2, 8, 129], BF16)
        gbc = sb("gbc", [128, 512], F32)
        dg = sb("dg", [128, 4, 128], F32)
        epsc = sb("epsc", [128, 1], F32)
        neglam = sb("neglam", [128, 1], F32)
        PS = es.enter_context(nc.psum_tensor("PS", [128, 7, 512], F32))
        PB = es.enter_context(nc.psum_tensor("PB", [128, 1024], BF16))
        P.track("PS")
        P.track("PB")
        P.track_whole("kv_loc")
        P.track_whole("kv_all")
        P.track_whole("mod_loc")
        P.track_whole("mod_all")

        xres = X[:, :].bitcast(F32).rearrange("p (t d) -> p t d", t=8)
        QT = X[:, 0:8192].rearrange("p (h t) -> p h t", h=8)
        aoT = X[:, 8192:16384].rearrange("p (h t) -> p h t", h=8)
        soT = X[:, 16384:24576].rearrange("p (h t) -> p h t", h=8)
        KTc = X[:, 24576:26624].rearrange("p (h t) -> p h t", h=8)
        hcT = X[:, 26624:30720].rearrange("p (k t) -> p k t", k=16)
        hxT = H[:, :].rearrange("p (k t) -> p k t", k=16)
        ctxf = M[:, 0:8192].bitcast(F32).rearrange("p (t d) -> p t d", t=2)
        KT_loc = M[:, 0:8192].rearrange("p (h t) -> p h t", h=8)
        V_loc = M[:, 8192:16384].rearrange("p (t c) -> p t c", t=8)
        uT = M[:, 0:8192].rearrange("p (h t) -> p h t", h=8)
        wsTb = M[:, 8192:9216].rearrange("p (g i) -> p g i", g=8)
        const2 = M[:, 9216:11264].bitcast(F32).rearrange("p (g i) -> p g i", g=8)
        Kr = [M[:, i * 1024:(i + 1) * 1024] for i in range(4)]
        Vr = [M[:, 4096 + i * 1032:4096 + (i + 1) * 1032].rearrange("p (t e) -> p t e", t=8) for i in range(4)]
        ET = [M[:, 8224 + j * 1024:8224 + (j + 1) * 1024] for j in range(4)]
        mT = M[:, :].rearrange("p (k t) -> p k t", k=16)
        fgbc = M[:, 0:4096].bitcast(F32)
        Sb = S[:, :].bitcast(BF16)
        xs = [S[:, 0:2048], S[:, 2048:4096]]
        ssq, std, rstd = stat[:, 0:16], stat[:, 16:32], stat[:, 32:48]
        bank = lambda b: PS[:, b, :]
        PBf = PB[:, :].bitcast(F32)

        V_BADA, V_C, V_CC, V_N1, V_N2, V_BG, V_LNG, V_LNB = 0, 96, 112, 128, 144, 160, 192, 200

        R = Ring(P, Wt, X[:, 8192:24576])
        full = lambda b: b.rearrange("p (k c) -> p k c", k=16)
        if not KFAST:
            ada_blk = [R.declare(2, [(full, wsrc(w_ada, 0, D, b * 512, 512))], unit=2 * b) for b in range(3)]
            inK = [R.declare(2, [(full, wsrc(w_in, 0, D, A + b * 512, 512))], unit=u) for b, u in ((0, 0), (1, 2))]
            inQ = [R.declare(2, [(full, wsrc(w_in, 0, D, b * 512, 512))], unit=u) for b, u in ((0, 4), (1, 8))]
            inV = [R.declare(2, [(full, wsrc(w_in, 0, D, 2 * A + b * 512, 512))], unit=u) for b, u in ((0, 6), (1, 0))]
            inU = [R.declare(2, [(full, wsrc(w_in, 0, D, 3 * A + b * 512, 512))], unit=u) for b, u in ((0, 2), (1, 8))]
            inVs = [R.declare(2, [(full, wsrc(w_in, 0, D, 4 * A + b * 512, 512))], unit=u) for b, u in ((0, 4), (1, 6))]
            half = lambda b: b.rearrange("p (k c) -> p k c", k=16)
            brv0 = lambda b: b[:, 0:2048].rearrange("p (k c) -> p k c", k=8)
            brv1 = lambda b: b[:, 2048:4096].rearrange("p (k c) -> p k c", k=8)
            mg_blk = []
            for st in range(8):
                ga = R.declare(1, [(half, wsrc(w_gate, 0, D, st * 256, 256))])
                gs = R.declare(1, [(half, wsrc(w_gate, 0, D, D + st * 256, 256))])
                br = R.declare(1, [(brv0, wsrc(w_bra, 0, A, st * 256, 256)), (brv1, wsrc(w_brs, 0, A, st * 256, 256))])
                mg_blk.append((ga, gs, br))

            def make_scale_post(chunk0):
                def post(base, chunk0=chunk0):
                    for j in range(4):
                        P.ts("dve", dg[:, j, :], idf[:, :], modT[:, chunk0 + j, 0:1], None, ALU.mult)
                    P.mm([(PBf[:, j * 128:(j + 1) * 128], onesf[:, :], dg[:, j, :], True, True) for j in range(4)])
                    P.copy("dve", gbc[:, :], PBf[:, :])
                    bv = full(base)
                    P.tt("pool", bv, bv, gbc[:, :].unsqueeze(1).to_broadcast([128, 16, 512]), ALU.mult)
                return post
            out_blk = [R.declare(2, [(full, wsrc(w_out, 0, D, nb * 512, 512))], post=make_scale_post(32 + nb * 4))
                       for nb in range(4)]
            ff_blk = []
            for fb in range(4):
                f1 = [R.declare(2, [(full, wsrc(w_ff1, 0, D, fb * 2048 + jb * 512, 512))]) for jb in range(4)]
                f2 = [R.declare(2, [(full, wsrc(w_ff2, fb * 2048, 2048, nb * 512, 512))], post=make_scale_post(80 + nb * 4))
                      for nb in range(4)]
                ff_blk.append((f1, f2))

        class _Stop(Exception):
            pass

        def _dump():
            if KDUMP == 'X':
                for tt in range(NT):
                    P.dma('sp', out_d[tt * 128:(tt + 1) * 128, :], xres[:, tt, :])
            else:
                hf = H[:, :].bitcast(F32).rearrange('p (t d) -> p t d', t=4)
                mf = M[:, :].bitcast(F32).rearrange('p (t d) -> p t d', t=4)
                for tt in range(4):
                    P.dma('sp', out_d[tt * 128:(tt + 1) * 128, :], hf[:, tt, :])
                    P.dma('sp', out_d[(4 + tt) * 128:(5 + tt) * 128, :], mf[:, tt, :])

        def _phases():
            P.dma("sp", vecs[:, :], vecs_d)
            P.dma("sp", idf[:, :], idn_d)
            P.dma("sp", gsub[:, :], subg_d)
            P.dma("sp", cos_sb[:, :, :].rearrange("p a b -> p (a b)"), cos_d)
            P.dma("sp", sin_sb[:, :, :].rearrange("p a b -> p (a b)"), sin_d)
            lam_sb = S[:, 2048:2304]
            P.dma("sp", lam_sb, lam_d)
            P.act(sT[:, :, 0], vecs[:, V_C:V_C + 16], AF.Silu)
            P.act(sT[:, :, 1], vecs[:, V_CC:V_CC + 16], AF.Silu)
            P.copy("dve", idb[:, :], idf[:, :])
            P.op("dve", "memset", [], [onesf[:, :]], onesf[:, :], 1.0)
            P.op("dve", "memset", [], [onesb[:, :]], onesb[:, :], 1.0)
            P.op("dve", "memset", [], [epsc[:, :]], epsc[:, :], EPS)
            P.op("dve", "memset", [], [Vc[:, :, :, 128:129]], Vc[:, :, :, 128:129], 1.0)
            P.ts("dve", gsub[:, :], gsub[:, :], 0.8, None, ALU.mult)
            ada_dep = []
            if not KFAST:
                R.pump()
                ada_dep = [Wt[:, 0:NUNIT * UNIT]]
            for tt in range(NT):
                P.dma("sp", xres[:, tt, :], x_d[tt * 128:(tt + 1) * 128, :], extra_reads=ada_dep)
            for j in range(2):
                P.dma("sp", ctxf[:, j, :], ctx_d[j * 128:(j + 1) * 128, :], extra_reads=ada_dep)
            p1_srcs = [xres[:, tt, :] for tt in range(NT)] + [ctxf[:, j, :] for j in range(2)]
            for i, src in enumerate(p1_srcs):
                P.act(xs[0], src, AF.Square, accum=ssq[:, i:i + 1])
            P.act(std[:, 0:10], ssq[:, 0:10], AF.Sqrt, bias=epsc[:, 0:1], scale=1.0 / D)
            lt = S[:, 2304:2432]
            P.tt("dve", lt[:, 0:64], lam_sb[:, 0:64], lam_sb[:, 64:128], ALU.mult)
            P.tt("dve", lt[:, 64:128], lam_sb[:, 128:192], lam_sb[:, 192:256], ALU.mult)
            P.op("dve", "reduce_sum", [lt[:, 0:128]], [stat[:, 48:50]], out=stat[:, 48:50],
                 in_=lt[:, 0:128].rearrange("p (a b) -> p a b", a=2), axis=AX.X)
            P.act(stat[:, 50:52], stat[:, 48:50], AF.Exp)
            P.tt("dve", stat[:, 52:53], stat[:, 51:52], stat[:, 50:51], ALU.subtract)
            P.ts("dve", neglam[:, :], stat[:, 52:53], -0.2, None, ALU.add)
            Mf2 = M[:, 8192:16384].bitcast(F32)
            modrow3 = Mf2[0:2, 0:1536]
            modall = Mf2[0:16, 0:1536]
            if KFAST:
                P.op('dve', 'memset', [], [modT[:, :, :]], modT[:, :, :], 0.5)
            else:
                for b in range(3):
                    base = full(R.get(ada_blk[b]))
                    P.mm([(PS[0:2, 6, 0:512], sT[:, kc, :], base[:, kc, :], kc == 0, kc == 15) for kc in range(16)])
                    P.copy("dve", modrow3[:, b * 512:(b + 1) * 512], PS[0:2, 6, 0:512])
                    R.release(ada_blk[b])
                P.dma("sp", mod_loc.ap(), modrow3)

                def cc_mod(e):
                    return e.collective_compute("AllGather", ALU.bypass, replica_groups=[list(range(NCORES))],
                                                ins=[mod_loc.ap().opt()], outs=[mod_all.ap().opt()])
                P.add("pool", "cc", cc_mod, [mod_loc.ap()], [mod_all.ap()])
                P.dma("sp", modall, mod_all.ap())
                P.mm([(PBf[:, q * 16:(q + 1) * 16], modall[:, q * 128:(q + 1) * 128], idf[0:16, 0:16], True, True)
                      for q in range(12)])
                pv = PBf[:, 0:192].rearrange("p (q x) -> p q x", q=12)
                for r in range(NCORES):
                    P.tt("dve", modT[:, r * 12:(r + 1) * 12, :], pv[:, :, 2 * r:2 * r + 2],
                         vecs[:, V_BADA + r * 12:V_BADA + (r + 1) * 12].unsqueeze(2).to_broadcast([128, 12, 2]), ALU.add)
            P.stt("dve", amod[:, 0, :], modT[:, 16:32, 0], 1.0, vecs[:, V_N1:V_N1 + 16], ALU.add, ALU.mult)
            P.stt("dve", amod[:, 1, :], modT[:, 16:32, 1], 1.0, vecs[:, V_N1:V_N1 + 16], ALU.add, ALU.mult)
            P.stt("dve", amod[:, 2, :], modT[:, 64:80, 0], 1.0, vecs[:, V_N2:V_N2 + 16], ALU.add, ALU.mult)

            if KSTOP <= 0:
                raise _Stop()
            def norm_stats(srcs, col0):
                n = len(srcs)
                for i, src in enumerate(srcs):
                    P.act(xs[0], src, AF.Square, accum=ssq[:, col0 + i:col0 + i + 1])
                if KSUB <= 0:
                    return
                P.act(std[:, col0:col0 + n], ssq[:, col0:col0 + n], AF.Sqrt, bias=epsc[:, 0:1], scale=1.0 / D)
                if KSUB <= 1:
                    return
                P.op("dve", "reciprocal", [std[:, col0:col0 + n]], [rstd[:, col0:col0 + n]],
                     out=rstd[:, col0:col0 + n], in_=std[:, col0:col0 + n])

            def norm_transpose(i, src, col, acol, shcol, dst):
                xb = xs[i % 2]
                if KSUB <= 2:
                    return
                P.ts("dve", xb, src, rstd[:, col:col + 1], None, ALU.mult)
                if KSUB <= 3:
                    return
                for g in range(4):
                    bk = (i * 4 + g) % 6
                    P.tr([(PS[:, bk, j * 128:(j + 1) * 128], xb[:, (g * 4 + j) * 128:(g * 4 + j + 1) * 128], idf[:, :])
                          for j in range(4)])
                    for j in range(4):
                        dc = g * 4 + j
                        if KSUB <= 4:
                            continue
                        if g % 2 == 0:
                            P.ts("dve", dst(dc), PS[:, bk, j * 128:(j + 1) * 128], acol(dc), shcol(dc), ALU.mult, ALU.add)
                        else:
                            P.act(dst(dc), PS[:, bk, j * 128:(j + 1) * 128], AF.Identity, bias=shcol(dc), scale=acol(dc))

            P.op("dve", "reciprocal", [std[:, 0:10]], [rstd[:, 0:10]], out=rstd[:, 0:10], in_=std[:, 0:10])
            for tt in range(NT):
                norm_transpose(tt, xres[:, tt, :], tt,
                               lambda dc: amod[:, 0, dc:dc + 1], lambda dc: modT[:, dc, 0:1],
                               lambda dc, tt=tt: hxT[:, dc, tt * 128:(tt + 1) * 128])
            for j in range(2):
                norm_transpose(NT + j, ctxf[:, j, :], NT + j,
                               lambda dc: amod[:, 1, dc:dc + 1], lambda dc: modT[:, dc, 1:2],
                               lambda dc, j=j: hcT[:, dc, j * 128:(j + 1) * 128])

            R.ext_ok = True
            if not KFAST:
                R.pump()
            if KSTOP <= 1:
                raise _Stop()
            rt = S[:, 0:1024].rearrange("p (a s i) -> p a s i", a=4, s=8)
            rot = [Sb[:, 2048:3072], Sb[:, 3072:4096]]
            pcnt = [0]

            def nbank():
                b = pcnt[0] % 6
                pcnt[0] += 1
                return b

            def proj_tm(lhs_fn, wbase, bk):
                P.mm([(bank(bk), lhs_fn(dc), wbase[:, dc, :], dc == 0, dc == 15) for dc in range(16)])

            def rope_evac(bk, tt, dst):
                ps = bank(bk).rearrange("p (s i two) -> p s i two", s=8, two=2)
                ev, od = ps[:, :, :, 0], ps[:, :, :, 1]
                cb = cos_sb[:, tt, :].unsqueeze(1).to_broadcast([128, 8, 32])
                sn = sin_sb[:, tt, :].unsqueeze(1).to_broadcast([128, 8, 32])
                P.tt("dve", rt[:, 0], ev, cb, ALU.mult)
                P.tt("dve", rt[:, 1], od, sn, ALU.mult)
                P.tt("dve", rt[:, 2], ev, sn, ALU.mult)
                P.tt("dve", rt[:, 3], od, cb, ALU.mult)
                dv = dst.rearrange("p (s i two) -> p s i two", s=8, two=2)
                P.tt("dve", dv[:, :, :, 0], rt[:, 0], rt[:, 1], ALU.subtract)
                P.tt("dve", dv[:, :, :, 1], rt[:, 2], rt[:, 3], ALU.add)

            def head_transposes(src, dst):
                P.tr([(PB[:, h * 128:(h + 1) * 128], src[:, h * 128:(h + 1) * 128], idb[:, :]) for h in range(8)])
                P.copy("act", dst, PB[:, :].rearrange("p (h t) -> p h t", h=8))

            kb = [full(R.get(inK[0])), full(R.get(inK[1]))]
            for tt in range(NT):
                r = rot[tt % 2]
                for b in range(2):
                    bk = nbank()
                    proj_tm(lambda dc, tt=tt: hxT[:, dc, tt * 128:(tt + 1) * 128], kb[b], bk)
                    rope_evac(bk, tt, r[:, b * 512:(b + 1) * 512])
                head_transposes(r, KT_loc[:, :, tt * 128:(tt + 1) * 128])
            P.dma("sp", kv_loc.ap()[0:1024, :].rearrange("(h e) t -> e h t", e=128), KT_loc)
            for j in range(2):
                r = rot[j % 2]
                for b in range(2):
                    bk = nbank()
                    proj_tm(lambda dc, j=j: hcT[:, dc, j * 128:(j + 1) * 128], kb[b], bk)
                    P.copy("dve", r[:, b * 512:(b + 1) * 512], bank(bk))
                head_transposes(r, KTc[:, :, j * 128:(j + 1) * 128])
            R.release(inK[0])
            R.release(inK[1])
            qb_ = [full(R.get(inQ[0])), full(R.get(inQ[1]))]
            for tt in range(NT):
                r = rot[tt % 2]
                for b in range(2):
                    bk = nbank()
                    proj_tm(lambda dc, tt=tt: hxT[:, dc, tt * 128:(tt + 1) * 128], qb_[b], bk)
                    rope_evac(bk, tt, r[:, b * 512:(b + 1) * 512])
                head_transposes(r, QT[:, :, tt * 128:(tt + 1) * 128])
            R.release(inQ[0])
            R.release(inQ[1])

            for b in range(2):
                vb = full(R.get(inV[b]))
                for tt in range(NT):
                    bk = nbank()
                    proj_tm(lambda dc, tt=tt: hxT[:, dc, tt * 128:(tt + 1) * 128], vb, bk)
                    P.copy("act", V_loc[:, tt, b * 512:(b + 1) * 512], bank(bk))
                for j in range(2):
                    bk = nbank()
                    proj_tm(lambda dc, j=j: hcT[:, dc, j * 128:(j + 1) * 128], vb, bk)
                    P.copy("act", Vc[:, j, b * 4:(b + 1) * 4, 0:128], bank(bk).rearrange("p (h e) -> p h e", h=4))
                R.release(inV[b])
            P.dma("sp", kv_loc.ap()[1024:2048, :].rearrange("(t p) c -> p t c", p=128), V_loc)

            def cc_fn(e):
                return e.collective_compute("AllGather", ALU.bypass, replica_groups=[list(range(NCORES))],
                                            ins=[kv_loc.ap().opt()], outs=[kv_all.ap().opt()])
            P.add("pool", "cc", cc_fn, [kv_loc.ap()], [kv_all.ap()])

            if KSTOP <= 2:
                raise _Stop()
            P.dma("pool", wsTb.rearrange("p g i -> p (g i)"), wsT_d)
            P.dma("sp", const2.rearrange("p g i -> p (g i)"), bsb_d)
            vg = S[:, 0:1024]
            sq = S[:, 1024:2048]
            zb = [Sb[:, 4096:5120], Sb[:, 5120:6144]]
            mtmp = S[:, 3072:4096].rearrange("p (g i) -> p g i", g=8)
            lns = stat[:, 53:64]
            for b in range(2):
                ub = full(R.get(inU[b]))
                for mc in range(4):
                    for tb in range(2):
                        bk = nbank()
                        P.mm([(bank(bk), ub[:, dc, mc * 128:(mc + 1) * 128], hxT[:, dc, tb * 512:(tb + 1) * 512],
                               dc == 0, dc == 15) for dc in range(16)])
                        P.act(uT[:, b * 4 + mc, tb * 512:(tb + 1) * 512], bank(bk), AF.Gelu)
                R.release(inU[b])
            P.mm([(PS[:, g // 4, (g % 4) * 128:(g % 4 + 1) * 128], onesb[:, :], wsTb[:, g, :], True, True) for g in range(8)])
            for g in range(8):
                P.stt("dve", const2[:, g, :], PS[:, g // 4, (g % 4) * 128:(g % 4 + 1) * 128],
                      vecs[:, V_LNB + g:V_LNB + g + 1], const2[:, g, :], ALU.mult, ALU.add)
            vsb = [full(R.get(inVs[0])), full(R.get(inVs[1]))]
            for tt in range(NT):
                bks = [0, 1] if tt % 2 == 0 else [2, 3]
                for b in range(2):
                    proj_tm(lambda dc, tt=tt: hxT[:, dc, tt * 128:(tt + 1) * 128], vsb[b], bks[b])
                P.act(vg, PS[:, bks[0]:bks[0] + 2, :].rearrange("p a b -> p (a b)"), AF.Gelu, accum=lns[:, 0:1])
                P.tt("dve", sq, vg, vg, ALU.mult)
                P.op("dve", "reduce_sum", [sq], [lns[:, 1:2]], out=lns[:, 1:2], in_=sq, axis=AX.X)
                P.ts("dve", lns[:, 2:3], lns[:, 0:1], 1.0 / 1024, None, ALU.mult)
                P.tt("dve", lns[:, 3:4], lns[:, 2:3], lns[:, 2:3], ALU.mult)
                P.stt("dve", lns[:, 4:5], lns[:, 1:2], 1.0 / 1024, lns[:, 3:4], ALU.mult, ALU.subtract)
                P.act(lns[:, 5:6], lns[:, 4:5], AF.Sqrt, bias=epsc[:, 0:1], scale=1.0)
                P.op("dve", "reciprocal", [lns[:, 5:6]], [lns[:, 6:7]], out=lns[:, 6:7], in_=lns[:, 5:6])
                z = zb[tt % 2]
                P.ts("dve", z, vg, lns[:, 2:3], lns[:, 6:7], ALU.subtract, ALU.mult)
                mb = [4, 5]
                P.mm([(PS[:, mb[g // 4], (g % 4) * 128:(g % 4 + 1) * 128], z[:, g * 128:(g + 1) * 128], wsTb[:, g, :], True, True)
                      for g in range(8)])
                for g in range(8):
                    P.stt("dve", mtmp[:, g, :], PS[:, mb[g // 4], (g % 4) * 128:(g % 4 + 1) * 128],
                          vecs[:, V_LNG + g:V_LNG + g + 1], const2[:, g, :], ALU.mult, ALU.add)
                P.tt("dve", soT[:, :, tt * 128:(tt + 1) * 128], mtmp, uT[:, :, tt * 128:(tt + 1) * 128], ALU.mult)
            R.release(inVs[0])
            R.release(inVs[1])

            if KSTOP <= 3:
                raise _Stop()
            for i in range(4):
                P.op("dve", "memset", [], [Vr[i][:, :, 128:129]], Vr[i][:, :, 128:129], 1.0)
            accs = S[:, 0:1032].rearrange("p (a e) -> p a e", a=8)
            o1 = S[:, 1040:1552].rearrange("p (j e) -> p j e", j=4)
            o2 = S[:, 1552:2064].rearrange("p (j e) -> p j e", j=4)
            o3 = S[:, 2064:2576].rearrange("p (j e) -> p j e", j=4)
            aotm = Sb[:, 5152:5664].rearrange("p (j e) -> p j e", j=4)
            fs = stat[:, 0:48]

            def acc_ap(idx):
                return PS[:, 4 + idx // 3, (idx % 3) * 129:(idx % 3 + 1) * 129]
            kvr = kv_all.ap()
            chunks = [(h, qb, r) for h in range(8) for qb in range(2) for r in range(8)]
            steps = []
            ci = 0
            for h in range(8):
                for qb in range(2):
                    for r in range(9):
                        cidx = None
                        if r < 8:
                            cidx = ci
                            ci += 1
                        for kt in range(8 if r < 8 else 2):
                            steps.append((h, qb, r, kt, cidx))
            loaded = [0]

            def ensure_loaded(upto):
                while loaded[0] <= min(upto, len(chunks) - 1):
                    c = loaded[0]
                    h, qb, r = chunks[c]
                    sl = c % 4
                    P.dma("sp", Kr[sl], kvr[r * 2048 + h * 128:r * 2048 + (h + 1) * 128, :])
                    P.dma("sp", Vr[sl][:, :, 0:128],
                          kvr[r * 2048 + 1024:(r + 1) * 2048, h * 128:(h + 1) * 128].rearrange("(t p) c -> p t c", p=128))
                    loaded[0] += 1

            sched = {}

            def finalize(h, qb, n):
                P.copy("dve", accs[:, 0:3, :], PS[:, 4, 0:387].rearrange("p (a e) -> p a e", a=3))
                P.copy("dve", accs[:, 3:6, :], PS[:, 5, 0:387].rearrange("p (a e) -> p a e", a=3))
                P.copy("dve", accs[:, 6:8, :], PS[:, 6, 0:258].rearrange("p (a e) -> p a e", a=2))
                P.op("dve", "reciprocal", [accs[:, :, 128]], [fs[:, 0:8]], out=fs[:, 0:8], in_=accs[:, :, 128])
                P.ts("dve", fs[:, 8:12], fs[:, 4:8], neglam[:, 0:1], None, ALU.mult)
                for j in range(4):
                    P.ts("dve", o1[:, j, :], accs[:, j, 0:128], fs[:, j:j + 1], None, ALU.mult)
                    P.stt("dve", o2[:, j, :], accs[:, 4 + j, 0:128], fs[:, 8 + j:9 + j], o1[:, j, :], ALU.mult, ALU.add)
                P.tt("dve", o3, o2, o2, ALU.mult)
                P.op("dve", "reduce_sum", [o3], [fs[:, 12:16]], out=fs[:, 12:16], in_=o3, axis=AX.X)

                def stage_b():
                    P.act(fs[:, 16:20], fs[:, 12:16], AF.Sqrt, bias=epsc[:, 0:1], scale=1.0 / 128)
                    P.op("dve", "reciprocal", [fs[:, 16:20]], [fs[:, 20:24]], out=fs[:, 20:24], in_=fs[:, 16:20])

                def stage_c(h=h, qb=qb):
                    for j in range(4):
                        P.stt("dve", aotm[:, j, :], o2[:, j, :], fs[:, 20 + j:21 + j], gsub[:, :], ALU.mult, ALU.mult)
                    P.tr([(PB[:, j * 128:(j + 1) * 128], aotm[:, j, :], idb[:, :]) for j in range(4)])
                    P.copy("dve", aoT[:, h, qb * 512:(qb + 1) * 512], PB[:, 0:512])
                sched.setdefault(n + 12, []).append(stage_b)
                sched.setdefault(n + 24, []).append(stage_c)

            def emit_S(n):
                h, qb, r, kt, cidx = steps[n]
                if cidx is not None and kt == 0:
                    ensure_loaded(cidx + 2)
                if cidx is not None:
                    sl = cidx % 4
                    kA = Kr[sl][0:64, kt * 128:(kt + 1) * 128]
                    kB = Kr[sl][64:128, kt * 128:(kt + 1) * 128]
                else:
                    kA = KTc[0:64, h, kt * 128:(kt + 1) * 128]
                    kB = KTc[64:128, h, kt * 128:(kt + 1) * 128]
                b0 = 2 * (n % 2)
                P.mm([(bank(b0), kA, QT[0:64, h, qb * 512:(qb + 1) * 512], True, True),
                      (bank(b0 + 1), kB, QT[64:128, h, qb * 512:(qb + 1) * 512], True, True)])
                P.act(ET[n % 4], PS[:, b0:b0 + 2, :].rearrange("p a b -> p (a b)"), AF.Exp, scale=0.125)

            def emit_PV(n):
                h, qb, r, kt, cidx = steps[n]
                vv = Vr[cidx % 4][:, kt, :] if cidx is not None else Vc[:, kt, h, :]
                et = ET[n % 4]
                first = (r == 0 and kt == 0)
                last = (r == 8 and kt == 1)
                P.mm([(acc_ap(s * 4 + j), et[:, s * 512 + j * 128:s * 512 + (j + 1) * 128], vv, first, last)
                      for s in range(2) for j in range(4)])
                if last:
                    finalize(h, qb, n + 1)

            for n in range(len(steps) + 2):
                if n < len(steps):
                    emit_S(n)
                if n >= 2:
                    emit_PV(n - 2)
                for fn in sched.pop(n, []):
                    fn()
            for k in sorted(sched):
                for fn in sched[k]:
                    fn()

            if KSTOP <= 4:
                raise _Stop()
            mgS = [S[:, i * 512:(i + 1) * 512] for i in range(8)]
            it = 0
            for st in range(8):
                ga, gs, br = mg_blk[st]
                gab = half(R.get(ga))
                gsb = half(R.get(gs))
                brb = R.get(br)
                bra, brs = brv0(brb), brv1(brb)
                for mc2 in range(2):
                    mch = st * 2 + mc2
                    for tb in range(2):
                        gset = [0, 1] if it % 2 == 0 else [2, 3]
                        sset = mgS[0:4] if it % 2 == 0 else mgS[4:8]
                        it += 1
                        tsl = slice(tb * 512, (tb + 1) * 512)
                        P.mm([(bank(gset[0]), gab[:, dc, mc2 * 128:(mc2 + 1) * 128], hxT[:, dc, tsl], dc == 0, dc == 15) for dc in range(16)])
                        P.mm([(bank(gset[1]), gsb[:, dc, mc2 * 128:(mc2 + 1) * 128], hxT[:, dc, tsl], dc == 0, dc == 15) for dc in range(16)])
                        P.mm([(bank(4), bra[:, ec, mc2 * 128:(mc2 + 1) * 128], aoT[:, ec, tsl], ec == 0, ec == 7) for ec in range(8)])
                        P.mm([(bank(5), brs[:, ec, mc2 * 128:(mc2 + 1) * 128], soT[:, ec, tsl], ec == 0, ec == 7) for ec in range(8)])
                        P.act(sset[0], bank(gset[0]), AF.Sigmoid, bias=vecs[:, V_BG + mch:V_BG + mch + 1])
                        P.act(sset[1], bank(gset[1]), AF.Sigmoid, bias=vecs[:, V_BG + 16 + mch:V_BG + 16 + mch + 1])
                        P.tt("dve", sset[2], sset[0], bank(4), ALU.mult)
                        P.tt("dve", sset[3], sset[1], bank(5), ALU.mult)
                        P.tt("dve", mT[:, mch, tsl], sset[2], sset[3], ALU.add)
                R.release(ga)
                R.release(gs)
                R.release(br)

            if KSTOP <= 5:
                raise _Stop()
            for tt in range(NT):
                P.dma("sp", xres[:, tt, :], x_d[tt * 128:(tt + 1) * 128, :])
            for nb in range(4):
                wb = full(R.get(out_blk[nb]))
                for tt in range(NT):
                    bk = nbank()
                    P.mm([(bank(bk), mT[:, mc, tt * 128:(tt + 1) * 128], wb[:, mc, :], mc == 0, mc == 15) for mc in range(16)])
                    P.tt("dve", xres[:, tt, nb * 512:(nb + 1) * 512], bank(bk), xres[:, tt, nb * 512:(nb + 1) * 512], ALU.add)
                R.release(out_blk[nb])

            if KSTOP <= 6:
                raise _Stop()
            norm_stats([xres[:, tt, :] for tt in range(NT)], 0)
            for tt in range(NT):
                norm_transpose(tt, xres[:, tt, :], tt,
                               lambda dc: amod[:, 2, dc:dc + 1], lambda dc: modT[:, 48 + dc, 0:1],
                               lambda dc, tt=tt: hxT[:, dc, tt * 128:(tt + 1) * 128])

            if KSTOP <= 7:
                raise _Stop()
            rl = [S[:, 0:512], S[:, 512:1024], S[:, 1024:1536], S[:, 1536:2048]]
            ri = 0
            for fb in range(4):
                f1, f2 = ff_blk[fb]
                for jb in range(4):
                    wb = full(R.get(f1[jb]))
                    for mc in range(4):
                        for tb in range(2):
                            bk = nbank()
                            tsl = slice(tb * 512, (tb + 1) * 512)
                            P.mm([(bank(bk), wb[:, dc, mc * 128:(mc + 1) * 128], hxT[:, dc, tsl], dc == 0, dc == 15) for dc in range(16)])
                            rr = rl[ri % 4]
                            ri += 1
                            P.act(rr, bank(bk), AF.Relu)
                            P.tt("dve", mT[:, jb * 4 + mc, tsl], rr, rr, ALU.mult)
                    R.release(f1[jb])
                for nb in range(4):
                    wb = full(R.get(f2[nb]))
                    for tt in range(NT):
                        bk = nbank()
                        P.mm([(bank(bk), mT[:, fc, tt * 128:(tt + 1) * 128], wb[:, fc, :], fc == 0, fc == 15) for fc in range(16)])
                        P.tt("dve", xres[:, tt, nb * 512:(nb + 1) * 512], bank(bk), xres[:, tt, nb * 512:(nb + 1) * 512], ALU.add)
                    R.release(f2[nb])

            if KSTOP <= 8:
                raise _Stop()
            P.dma("sp", fgbc, fing_d)
            norm_stats([xres[:, tt, :] for tt in range(NT)], 0)
            for tt in range(NT):
                P.stt("dve", xres[:, tt, :], xres[:, tt, :], rstd[:, tt:tt + 1], fgbc, ALU.mult, ALU.mult)
                P.dma("sp", out_d[tt * 128:(tt + 1) * 128, :], xres[:, tt, :])


        try:
            _phases()
        except _Stop:
            _dump()

        P.finalize()
        esem = {e: es.enter_context(nc.semaphore("e_" + e)) for e in Prog.ENG}
        dsem = {}
        for q in ("sp", "pool"):
            for k in range(NSEM_DMA):
                dsem[(q, k)] = es.enter_context(nc.semaphore("d_%s_%d" % (q, k)))
        ccsem = es.enter_context(nc.semaphore("cc"))
        block = es.enter_context(nc.Block())

        @block.sync
        def _(e):
            P.emit("sp", e, esem, dsem, ccsem)

        @block.gpsimd
        def _(e):
            P.emit("pool", e, esem, dsem, ccsem)

        @block.scalar
        def _(e):
            P.emit("act", e, esem, dsem, ccsem)

        @block.vector
        def _(e):
            P.emit("dve", e, esem, dsem, ccsem)

        @block.tensor
        def _(e):
            P.emit("pe", e, esem, dsem, ccsem)
    return nc


def _col(v):
    v = np.asarray(v, np.float32).reshape(-1, 128)
    return np.ascontiguousarray(v.T)


def _rope_tables():
    n_freq = 16
    inv = (np.float32(10000.0) ** (-np.arange(n_freq, dtype=np.float32) / np.float32(n_freq))).astype(np.float32)
    tok = np.arange(NCORES * T)
    r = (tok // 64).astype(np.float32)[:, None]
    c = (tok % 64).astype(np.float32)[:, None]
    ang = np.concatenate([r * inv[None, :], c * inv[None, :]], axis=-1).astype(np.float32)
    return np.cos(ang).astype(np.float32), np.sin(ang).astype(np.float32)


_NC_CACHE = {}


def kernel(x, c, ctx, c_ctx, w_ada, b_ada, norm1_g, norm2_g, w_in, lam_q1, lam_k1, lam_q2, lam_k2,
           subln_g, sgu_ln_g, sgu_ln_b, w_spatial, b_spatial, w_gate, b_gate, w_br_attn, w_br_sgu,
           w_out, w_ff1, w_ff2, final_g):
    f = lambda a: np.ascontiguousarray(np.asarray(a, dtype=np.float32))
    x2 = f(x).reshape(NCORES * T, D)
    ctx2 = f(ctx).reshape(CT, D)
    vecs = np.concatenate([_col(b_ada), _col(c), _col(c_ctx), _col(norm1_g), _col(norm2_g), _col(b_gate),
                           _col(sgu_ln_g), _col(sgu_ln_b)], axis=1)
    vecs = np.ascontiguousarray(vecs, dtype=np.float32)
    assert vecs.shape == (128, 208)
    lamv = np.concatenate([f(lam_q1).reshape(-1), f(lam_k1).reshape(-1), f(lam_q2).reshape(-1), f(lam_k2).reshape(-1)])
    lamv = np.ascontiguousarray(np.broadcast_to(lamv[None, :], (128, 256)))
    subg = np.ascontiguousarray(np.broadcast_to(f(subln_g).reshape(1, 128), (128, 128)))
    fing = np.ascontiguousarray(np.broadcast_to(f(final_g).reshape(1, D), (128, D)))
    ws = f(w_spatial).reshape(8, 128, 128)
    wsT = np.ascontiguousarray(ws.transpose(2, 0, 1).reshape(128, 8 * 128))
    bs = f(b_spatial).reshape(8 * 128)
    bsb = np.ascontiguousarray(np.broadcast_to(bs[None, :], (128, 8 * 128)))
    cos, sin = _rope_tables()
    idn = np.eye(128, dtype=np.float32)
    shared = dict(ctx=ctx2, vecs=vecs, lamv=lamv, subg=subg, fing=fing, wsT=wsT, bsb=bsb, idn=idn,
                  w_in=f(w_in).reshape(D, 5 * A), w_gate=f(w_gate).reshape(D, 2 * D),
                  w_br_attn=f(w_br_attn).reshape(A, D), w_br_sgu=f(w_br_sgu).reshape(A, D),
                  w_out=f(w_out).reshape(D, D), w_ff1=f(w_ff1).reshape(D, 4 * D), w_ff2=f(w_ff2).reshape(4 * D, D))
    if int(os.environ.get('KFAST', '0')):
        for k in list(shared):
            if k.startswith('w_'):
                shared[k] = np.zeros((128, 128), np.float32)
    wa = f(w_ada).reshape(D, 6 * D)
    kfast = int(os.environ.get('KFAST', '0'))
    in_maps = []
    for r in range(NCORES):
        m = dict(shared)
        m["x"] = np.ascontiguousarray(x2[r * T:(r + 1) * T])
        m["w_ada"] = np.ascontiguousarray(wa[:, r * 1536:(r + 1) * 1536]) if not kfast else np.zeros((128, 128), np.float32)
        cs = cos[r * T:(r + 1) * T].reshape(8, 128, 32).transpose(1, 0, 2).reshape(128, 256)
        sn = sin[r * T:(r + 1) * T].reshape(8, 128, 32).transpose(1, 0, 2).reshape(128, 256)
        m["ropec"] = np.ascontiguousarray(cs)
        m["ropes"] = np.ascontiguousarray(sn)
        in_maps.append(m)
    if "nc" not in _NC_CACHE:
        _NC_CACHE["nc"] = build_nc()
    res = run_bass_kernel_spmd(_NC_CACHE["nc"], in_maps, core_ids=list(range(NCORES)))
    out = np.concatenate([res.results[r]["out"] for r in range(NCORES)], axis=0)
    return out.reshape(1, NCORES * T, D).astype(np.float32)
```
